# Optimizing a Trainium2 kernel written in Bass

```python
import math
import jax, jax.numpy as jnp
from jax import lax
import numpy as np

D_MODEL = 1024
BATCH = 4
SEQ = 4096
DEPTH = 4

CHUNK = 64
HEAD_DIM = 64
N_SELF_HEADS = 12
N_DIFF_HEADS = 6
N_MEM_HEADS = 4
N_MEM = 256
SELF_WIDTH = N_SELF_HEADS * HEAD_DIM
MEM_WIDTH = N_MEM_HEADS * HEAD_DIM
IN_WIDTH = 3 * SELF_WIDTH + MEM_WIDTH
LEFT_CHUNKS = 8
BAND = (LEFT_CHUNKS + 1) * CHUNK
REL_CLIP = 128
N_REL = 2 * REL_CLIP + 1
Q_BLOCK = 128
D_FF = 2816
ROPE_THETA = 10000.0
EPS = 1e-6
NEG_INF = -1e30
N_MIXERS = 2
N_A_LAYERS = (DEPTH + 1) // 2
N_B_LAYERS = DEPTH // 2

kernel_name = 'hybrid_chunked_diff_memory_macaron'


def rmsnorm(x, g):
    x32 = x.astype(jnp.float32)
    y = x32 * lax.rsqrt(jnp.mean(x32 * x32, axis=-1, keepdims=True) + EPS)
    return (y * g.astype(jnp.float32)).astype(x.dtype)


def swiglu(h, w_gate, w_up, w_down):
    return (jax.nn.silu(h @ w_gate) * (h @ w_up)) @ w_down


def rope_tables(positions):
    inv_freq = 1.0 / (ROPE_THETA ** (jnp.arange(0, HEAD_DIM, 2, dtype=jnp.float32) / HEAD_DIM))
    ang = positions.astype(jnp.float32)[..., None] * inv_freq
    return jnp.cos(ang), jnp.sin(ang)


def apply_rope(x, cos, sin):
    x32 = x.astype(jnp.float32)
    x1, x2 = jnp.split(x32, 2, axis=-1)
    return jnp.concatenate([x1 * cos - x2 * sin, x2 * cos + x1 * sin], axis=-1).astype(x.dtype)


def chunked_rel_attention(q, k, v, rel_bias):
    B, S, H, D = q.shape
    n_chunks = S // CHUNK
    pad = ((0, 0), (LEFT_CHUNKS * CHUNK, 0), (0, 0), (0, 0))
    kp = jnp.pad(k, pad)
    vp = jnp.pad(v, pad)
    qi = jnp.arange(CHUNK)[:, None]
    ks = jnp.arange(BAND)[None, :]
    rel = LEFT_CHUNKS * CHUNK + qi - ks
    bias = rel_bias[:, jnp.clip(rel, -REL_CLIP, REL_CLIP) + REL_CLIP].astype(jnp.float32)
    scale = HEAD_DIM ** -0.5
    slot = jnp.arange(BAND)

    def one_chunk(c):
        start = c * CHUNK
        qc = lax.dynamic_slice_in_dim(q, start, CHUNK, axis=1)
        kc = lax.dynamic_slice_in_dim(kp, start, BAND, axis=1)
        vc = lax.dynamic_slice_in_dim(vp, start, BAND, axis=1)
        s = jnp.einsum('bqhd,bkhd->bhqk', qc, kc).astype(jnp.float32) * scale + bias
        valid = slot >= (LEFT_CHUNKS - c) * CHUNK
        s = jnp.where(valid, s, NEG_INF)
        p = jax.nn.softmax(s, axis=-1).astype(v.dtype)
        return jnp.einsum('bhqk,bkhd->bqhd', p, vc)

    out = lax.map(one_chunk, jnp.arange(n_chunks))
    return out.transpose(1, 0, 2, 3, 4).reshape(B, S, H, D)


def diff_attention(q, k, v, lam):
    B, S, H, _, D = q.shape
    n_blocks = S // Q_BLOCK
    scale = D ** -0.5
    k_chunk = jnp.arange(S) // CHUNK
    qb = q.reshape(B, n_blocks, Q_BLOCK, H, 2, D).transpose(1, 0, 2, 3, 4, 5)

    def one_block(args):
        q_blk, b = args
        s = jnp.einsum('bqhcd,bkhcd->bhcqk', q_blk, k).astype(jnp.float32) * scale
        q_chunk = (b * Q_BLOCK + jnp.arange(Q_BLOCK)) // CHUNK
        allowed = k_chunk[None, :] <= q_chunk[:, None]
        s = jnp.where(allowed, s, NEG_INF)
        p = jax.nn.softmax(s, axis=-1)
        a = (p[:, :, 0] - lam * p[:, :, 1]).astype(v.dtype)
        return jnp.einsum('bhqk,bkhe->bqhe', a, v)

    out = lax.map(one_block, (qb, jnp.arange(n_blocks)))
    return out.transpose(1, 0, 2, 3, 4).reshape(B, S, H, v.shape[-1])


def memory_attention(q, k, v):
    s = jnp.einsum('bshd,bnhd->bhsn', q, k).astype(jnp.float32) * (HEAD_DIM ** -0.5)
    p = jax.nn.softmax(s, axis=-1).astype(v.dtype)
    return jnp.einsum('bhsn,bnhd->bshd', p, v)


def setup_inputs(seed: int = 0) -> dict:
    key = jax.random.key(seed)
    ks = iter(jax.random.split(key, 40))
    f32 = jnp.float32

    def nrm(shape, scale):
        return jax.random.normal(next(ks), shape, f32) * scale

    def gain(shape):
        return 1.0 + 0.02 * jax.random.normal(next(ks), shape, f32)

    x = nrm((BATCH, SEQ, D_MODEL), 1.0)
    mem = nrm((BATCH, N_MEM, D_MODEL), 1.0)
    offset = jax.random.randint(next(ks), (BATCH, 1), 0, SEQ, dtype=jnp.int32)
    positions = offset + jnp.arange(SEQ, dtype=jnp.int32)[None, :]
    return {
        'x': x,
        'mem': mem,
        'positions': positions,
        'ffn1_norm': gain((DEPTH, D_MODEL)),
        'ffn1_w_gate': nrm((DEPTH, D_MODEL, D_FF), D_MODEL ** -0.5),
        'ffn1_w_up': nrm((DEPTH, D_MODEL, D_FF), D_MODEL ** -0.5),
        'ffn1_w_down': nrm((DEPTH, D_FF, D_MODEL), D_FF ** -0.5),
        'mix_norm': gain((DEPTH, D_MODEL)),
        'mem_norm': gain((DEPTH, D_MODEL)),
        'w_in': nrm((DEPTH, D_MODEL, IN_WIDTH), D_MODEL ** -0.5),
        'w_mem_kv': nrm((DEPTH, D_MODEL, 2 * MEM_WIDTH), D_MODEL ** -0.5),
        'mem_q_norm': gain((DEPTH, HEAD_DIM)),
        'mem_k_norm': gain((DEPTH, HEAD_DIM)),
        'w_out': nrm((DEPTH, SELF_WIDTH + MEM_WIDTH, D_MODEL), (SELF_WIDTH + MEM_WIDTH) ** -0.5),
        'a_q_norm': gain((N_A_LAYERS, HEAD_DIM)),
        'a_k_norm': gain((N_A_LAYERS, HEAD_DIM)),
        'a_rel_bias': nrm((N_A_LAYERS, N_SELF_HEADS, N_REL), 0.1),
        'b_q_norm': gain((N_B_LAYERS, HEAD_DIM)),
        'b_k_norm': gain((N_B_LAYERS, HEAD_DIM)),
        'b_lambda_q1': nrm((N_B_LAYERS, HEAD_DIM), 0.1),
        'b_lambda_k1': nrm((N_B_LAYERS, HEAD_DIM), 0.1),
        'b_lambda_q2': nrm((N_B_LAYERS, HEAD_DIM), 0.1),
        'b_lambda_k2': nrm((N_B_LAYERS, HEAD_DIM), 0.1),
        'b_subln': gain((N_B_LAYERS, 2 * HEAD_DIM)),
        'ffn2_norm': gain((DEPTH, D_MODEL)),
        'ffn2_w_gate': nrm((DEPTH, D_MODEL, D_FF), D_MODEL ** -0.5),
        'ffn2_w_up': nrm((DEPTH, D_MODEL, D_FF), D_MODEL ** -0.5),
        'ffn2_w_down': nrm((DEPTH, D_FF, D_MODEL), D_FF ** -0.5),
    }


def reference(x, mem, positions, ffn1_norm, ffn1_w_gate, ffn1_w_up, ffn1_w_down,
              mix_norm, mem_norm, w_in, w_mem_kv, mem_q_norm, mem_k_norm, w_out,
              a_q_norm, a_k_norm, a_rel_bias, b_q_norm, b_k_norm,
              b_lambda_q1, b_lambda_k1, b_lambda_q2, b_lambda_k2, b_subln,
              ffn2_norm, ffn2_w_gate, ffn2_w_up, ffn2_w_down):
    B, S, _ = x.shape
    cos, sin = rope_tables(positions)
    cos = cos[:, :, None, None, :]
    sin = sin[:, :, None, None, :]

    for i in range(DEPTH):
        h = rmsnorm(x, ffn1_norm[i])
        x = x + 0.5 * swiglu(h, ffn1_w_gate[i], ffn1_w_up[i], ffn1_w_down[i])

        h = rmsnorm(x, mix_norm[i])
        proj = h @ w_in[i]
        q_s, k_s, v_s, q_m = jnp.split(proj, [SELF_WIDTH, 2 * SELF_WIDTH, 3 * SELF_WIDTH], axis=-1)

        mkv = rmsnorm(mem, mem_norm[i]) @ w_mem_kv[i]
        k_m, v_m = jnp.split(mkv, 2, axis=-1)
        q_m = rmsnorm(q_m.reshape(B, S, N_MEM_HEADS, HEAD_DIM), mem_q_norm[i])
        k_m = rmsnorm(k_m.reshape(B, N_MEM, N_MEM_HEADS, HEAD_DIM), mem_k_norm[i])
        v_m = v_m.reshape(B, N_MEM, N_MEM_HEADS, HEAD_DIM)
        o_m = memory_attention(q_m, k_m, v_m).reshape(B, S, MEM_WIDTH)

        j = i // N_MIXERS
        if i % N_MIXERS == 0:
            q = rmsnorm(q_s.reshape(B, S, N_SELF_HEADS, HEAD_DIM), a_q_norm[j])
            k = rmsnorm(k_s.reshape(B, S, N_SELF_HEADS, HEAD_DIM), a_k_norm[j])
            v = v_s.reshape(B, S, N_SELF_HEADS, HEAD_DIM)
            o_s = chunked_rel_attention(q, k, v, a_rel_bias[j]).reshape(B, S, SELF_WIDTH)
        else:
            lambda_init = 0.8 - 0.6 * math.exp(-0.3 * i)
            q = rmsnorm(q_s.reshape(B, S, N_DIFF_HEADS, 2, HEAD_DIM), b_q_norm[j])
            k = rmsnorm(k_s.reshape(B, S, N_DIFF_HEADS, 2, HEAD_DIM), b_k_norm[j])
            q = apply_rope(q, cos, sin)
            k = apply_rope(k, cos, sin)
            v = v_s.reshape(B, S, N_DIFF_HEADS, 2 * HEAD_DIM)
            lam = (jnp.exp(jnp.sum(b_lambda_q1[j].astype(jnp.float32) * b_lambda_k1[j].astype(jnp.float32)))
                   - jnp.exp(jnp.sum(b_lambda_q2[j].astype(jnp.float32) * b_lambda_k2[j].astype(jnp.float32)))
                   + lambda_init)
            o = diff_attention(q, k, v, lam)
            o = rmsnorm(o, b_subln[j]) * (1.0 - lambda_init)
            o_s = o.reshape(B, S, SELF_WIDTH)

        x = x + jnp.concatenate([o_s, o_m], axis=-1) @ w_out[i]

        h = rmsnorm(x, ffn2_norm[i])
        x = x + 0.5 * swiglu(h, ffn2_w_gate[i], ffn2_w_up[i], ffn2_w_down[i])
    return x
```

```python
import numpy as np
import concourse.bass as bass
import concourse.mybir as mybir
from concourse.bass_utils import run_bass_kernel_spmd

F32 = mybir.dt.float32
BF16 = mybir.dt.bfloat16
I32 = mybir.dt.int32
AF = mybir.ActivationFunctionType
ALU = mybir.AluOpType


class Res:
    __slots__ = ("name", "w", "r")

    def __init__(self, name=""):
        self.name = name
        self.w = None
        self.r = {}


class Prog:
    ENGS = ("pe", "act", "dve", "pool", "sp")

    def __init__(self, nc):
        self.nc = nc
        self.ops = {e: [] for e in self.ENGS}
        self.tick = {e: 0 for e in self.ENGS}
        self.seen = {e: {} for e in self.ENGS}
        self.dma_val = {}
        self.sem_names = list(self.ENGS[:4])
        self.sems = {}

    def new_dma_sem(self, name):
        assert name not in self.dma_val
        self.dma_val[name] = 0
        self.sem_names.append(name)
        return name

    def _collect(self, eng, reads, writes):
        need = {}

        def add(tok):
            if tok is None:
                return
            k, v = tok
            if need.get(k, 0) < v:
                need[k] = v

        for r in reads:
            add(r.w)
        for w in writes:
            add(w.w)
            for k, v in w.r.items():
                add((k, v))
        out = []
        for k, v in need.items():
            if k == "pe" and eng == "pe":
                continue
            if self.seen[eng].get(k, 0) >= v:
                continue
            self.seen[eng][k] = v
            out.append((k, v))
        return out

    def _commit(self, tok, reads, writes):
        k, v = tok
        for r in reads:
            if r.r.get(k, 0) < v:
                r.r[k] = v
        for w in writes:
            w.w = tok
            w.r = {}

    def op(self, eng, emit, reads=(), writes=()):
        waits = self._collect(eng, reads, writes)
        self.tick[eng] += 1
        tok = (eng, self.tick[eng])
        self.ops[eng].append((waits, emit, eng, 1))
        self._commit(tok, reads, writes)
        return tok

    def dma(self, eng, sem, emit, reads=(), writes=(), inc=16):
        waits = self._collect(eng, reads, writes)
        self.dma_val[sem] += inc
        tok = (sem, self.dma_val[sem])
        self.ops[eng].append((waits, emit, sem, inc))
        self._commit(tok, reads, writes)
        return tok

    def wait_all(self, eng, toks):
        waits = []
        for k, v in toks:
            if self.seen[eng].get(k, 0) < v:
                self.seen[eng][k] = v
                waits.append((k, v))
        self.ops[eng].append((waits, None, None, 0))

    def emit(self):
        nc = self.nc
        from contextlib import ExitStack
        with ExitStack() as st:
            for n in self.sem_names:
                self.sems[n] = st.enter_context(nc.semaphore("s_" + n))
            block = st.enter_context(nc.Block())

            def replay(name):
                def body(e):
                    for waits, emit, semk, inc in self.ops[name]:
                        for k, v in waits:
                            e.wait_ge(self.sems[k], v)
                        if emit is not None:
                            emit(e).then_inc(self.sems[semk], inc)
                return body

            block.tensor(replay("pe"))
            block.scalar(replay("act"))
            block.vector(replay("dve"))
            block.gpsimd(replay("pool"))
            block.sync(replay("sp"))


NT = 2048
TB = 512
NTB = 4
DC = 8
FF = 2816
NGF = 11
EPS = 1e-6
MASKV = -240000.0
NKT = 36
G_FFN1, G_MIX, G_FFN2, G_MEM, G_Q, G_K, G_MQ, G_MK, G_SUB = 0, 8, 16, 24, 32, 33, 34, 35, 36
NGC = 37
C_ONES, C_BD, C_ROT, C_JREV = 0, 1, 2, 3
R_ONES, R_KLO, R_KHI, R_QLO, R_QHI, R_QALL, R_PAD0, R_B3 = 0, 1, 2, 3, 4, 5, 6, 14
NROW = 15


class Ctx:
    pass


def build_program(phase, mixer):
    from contextlib import ExitStack
    nc = bass.Bass("TRN2", target_bir_lowering=False)
    P = Prog(nc)
    K = Ctx()
    K.nc, K.P = nc, P
    st = ExitStack()

    def din(name, shape, dt=F32):
        return nc.dram_tensor(name, list(shape), dt, kind="ExternalInput").ap()

    def dout(name, shape, dt=F32):
        return nc.dram_tensor(name, list(shape), dt, kind="ExternalOutput").ap()

    def sb(name, shape, dt):
        return st.enter_context(nc.sbuf_tensor(name, list(shape), dt))

    xT_in = din("xT", [1024, NT])
    xT_out = dout("xT_out", [1024, NT])
    gains_d = din("gains", [128, NGC])
    cmat_d = din("cmat", [128, 4 * 128], BF16)
    rows_d = din("rows", [1, NROW * 128], BF16)
    if phase == "A":
        wg_d = din("wg", [1024, FF]); wu_d = din("wu", [1024, FF]); wd_d = din("wd", [FF, 1024])
        win_d = din("w_in", [1024, 2560])
        QT_out = dout("QT", [1024, NT], BF16)
        KT_out = dout("KT", [768, NT], BF16)
        V_out = dout("V", [NT, 768], BF16)
        if mixer == "b":
            pos_d = din("pos", [128, NT], I32)
            invf_d = din("invf", [128, 1])
    else:
        wg_d = din("wg", [1024, FF]); wu_d = din("wu", [1024, FF]); wd_d = din("wd", [FF, 1024])
        wout_d = din("w_out", [1024, 1024])
        wkv_d = din("w_kv", [1024, 512])
        memT_d = din("memT", [1024, 256])
        QT_in = din("QT", [1024, NT], BF16)
        KTp_d = din("KTp", [768, NKT * 128], BF16)
        Vp_d = din("Vp", [NKT * 128, 768], BF16)
        if mixer == "a":
            rb_d = din("relb", [12, 257])
            E_d = nc.dram_tensor("E_scratch", [12, 768], F32).ap()
        else:
            lamv_d = din("lamv", [1, 256])
            lconst_d = din("lconst", [128, 2])

    NSLOT = 3 if phase == "A" else 2
    x = sb("x", [128, DC, NT], F32)
    hT = sb("hT", [128, DC, NT], BF16)
    ring = [sb(f"ring{i}", [128, 6144], BF16) for i in range(NSLOT)]
    sq = sb("sq", [128, DC, TB], BF16)
    gains = sb("gains_sb", [128, NGC], F32)
    cmat = sb("cmat_sb", [128, 4, 128], BF16)
    rows = sb("rows_sb", [1, NROW * 128], BF16)
    rstd = [sb(f"rstd{i}", [128, TB], F32) for i in range(2)]
    sg = [sb(f"sg{i}", [128, TB], F32) for i in range(2)]
    actb = [sb(f"actb{i}", [128, 2, TB], BF16) for i in range(2)]
    ps = [st.enter_context(nc.psum_tensor(f"ps{i}", [128, 512], F32)) for i in range(8)]

    r_x = [[Res(f"x{c}_{t}") for t in range(NTB)] for c in range(DC)]
    r_h = [[Res(f"h{c}_{t}") for t in range(NTB)] for c in range(DC)]
    r_ring = [Res(f"ring{i}") for i in range(NSLOT)]
    s_ring = [P.new_dma_sem(f"ring{i}") for i in range(NSLOT)]
    r_sq = Res("sq"); r_gains = Res("gains"); r_cmat = Res("cmat"); r_rows = Res("rows")
    r_rstd = [Res(), Res()]; r_sg = [Res(), Res()]
    r_act = [[Res(), Res()], [Res(), Res()]]
    r_ps = [Res(f"ps{i}") for i in range(8)]
    ring_i = [0]
    cnt = {"rstd": 0, "sg": 0, "act": 0}

    def tbs(t):
        return slice(t * TB, (t + 1) * TB)

    def mm(out, lhsT, rhs, start, stop, reads, writes):
        P.op("pe", lambda e, o=out, l=lhsT, r=rhs, s0=start, s1=stop: e.matmul(o, l, r, start=s0, stop=s1),
             reads=reads, writes=writes)

    def act(out, in_, func, reads, writes, scale=1.0, bias=0.0):
        P.op("act", lambda e, o=out, i=in_, f=func, s=scale, b=bias: e.activation(out=o, in_=i, func=f, bias=b, scale=s),
             reads=reads, writes=writes)

    def stt(eng, out, in0, scalar, in1, op0, op1, reads, writes):
        P.op(eng, lambda e, o=out, a=in0, s=scalar, b=in1, p0=op0, p1=op1: e.scalar_tensor_tensor(out=o, in0=a, scalar=s, in1=b, op0=p0, op1=p1),
             reads=reads, writes=writes)

    def tt(eng, out, in0, in1, op, reads, writes):
        P.op(eng, lambda e, o=out, a=in0, b=in1, p=op: e.tensor_tensor(out=o, in0=a, in1=b, op=p),
             reads=reads, writes=writes)

    def recip(out, in_, reads, writes):
        P.op("dve", lambda e, o=out, i=in_: e.reciprocal(out=o, in_=i), reads=reads, writes=writes)

    def dma(eng, sem, out, in_, reads, writes, slow=False):
        if slow:
            P.dma(eng, sem, lambda e, o=out, i=in_: e.dma_start(out=o, in_=i, allow_slow_non_contiguous=True), reads=reads, writes=writes)
        else:
            P.dma(eng, sem, lambda e, o=out, i=in_: e.dma_start(out=o, in_=i), reads=reads, writes=writes)

    s_x = P.new_dma_sem("xload")
    s_c = P.new_dma_sem("consts")
    dma("sp", s_c, gains[:], gains_d, [], [r_gains])
    dma("sp", s_c, cmat[:], cmat_d.rearrange("p (k f) -> p k f", k=4), [], [r_cmat])
    dma("sp", s_c, rows[:], rows_d, [], [r_rows])
    r_gains.w = r_cmat.w = r_rows.w = (s_c, P.dma_val[s_c])
    xv = xT_in.rearrange("(c p) n -> p c n", p=128)

    def load_x(extra_w=()):
        for c in range(DC):
            dma("sp", s_x, x[:, c, :], xv[:, c, :], [], [r_x[c][t] for t in range(NTB)] + list(extra_w))
        for c in range(DC):
            for t in range(NTB):
                r_x[c][t].w = (s_x, P.dma_val[s_x])

    if phase == "A":
        load_x()
    ones_m = cmat[:, C_ONES, :]
    bd_m = cmat[:, C_BD, :]
    rot_m = cmat[:, C_ROT, :]
    jrev_m = cmat[:, C_JREV, :]

    def row(k, n=128):
        return rows[0:1, k * 128:k * 128 + n]

    def make_rstd(ps_ap, r_psum, n, dim, np_=128):
        k = cnt["rstd"] % 2
        cnt["rstd"] += 1
        act(rstd[k][0:np_, 0:n], ps_ap, AF.Sqrt, [r_psum], [r_rstd[k]], scale=1.0 / dim, bias=EPS)
        recip(rstd[k][0:np_, 0:n], rstd[k][0:np_, 0:n], [r_rstd[k]], [r_rstd[k]])
        return rstd[k], r_rstd[k]

    def rmsnorm_tb(src, r_src, dst, r_dst, gcol, n, ssbank):
        for c in range(DC):
            act(sq[:, c, 0:n], src(c), AF.Square, [r_src(c)], [r_sq])
        for c in range(DC):
            mm(ps[ssbank][:, 0:n], ones_m, sq[:, c, 0:n], c == 0, c == DC - 1, [r_sq, r_cmat], [r_ps[ssbank]])
        rs, r_rs = make_rstd(ps[ssbank][:, 0:n], r_ps[ssbank], n, 1024.0)
        for c in range(DC):
            stt("dve", dst(c), src(c), gains[:, gcol + c:gcol + c + 1], rs[:, 0:n], ALU.mult, ALU.mult,
                [r_src(c), r_rs, r_gains], [r_dst(c)])

    def ring_next():
        i = ring_i[0] % NSLOT
        ring_i[0] += 1
        return i

    def ffn(wg, wu, wd, gcol):
        for t in range(NTB):
            rmsnorm_tb(lambda c, t=t: x[:, c, tbs(t)], lambda c, t=t: r_x[c][t],
                       lambda c, t=t: hT[:, c, tbs(t)], lambda c, t=t: r_h[c][t], gcol, TB, 7)
        wgv = wg.rearrange("(c p) f -> p c f", p=128)
        wuv = wu.rearrange("(c p) f -> p c f", p=128)
        wdv = wd.rearrange("(j p) d -> p j d", p=128)
        slots = {}

        def issue(g):
            i = ring_next()
            slots[g] = i
            rv = ring[i]
            dma("pool", s_ring[i], rv[:, 0:2048].rearrange("p (c f) -> p c f", c=8), wgv[:, :, g * 256:(g + 1) * 256], [], [r_ring[i]])
            dma("pool", s_ring[i], rv[:, 2048:4096].rearrange("p (c f) -> p c f", c=8), wuv[:, :, g * 256:(g + 1) * 256], [], [r_ring[i]])
            dma("pool", s_ring[i], rv[:, 4096:6144].rearrange("p (j d) -> p j d", j=2), wdv[:, 2 * g:2 * g + 2, :], [], [r_ring[i]])

        for g in range(min(NSLOT - 1, NGF)):
            issue(g)
        pdi = 0
        for g in range(NGF):
            if g + NSLOT - 1 < NGF:
                issue(g + NSLOT - 1)
            i = slots[g]
            rv = ring[i]
            wgs = rv[:, 0:2048].rearrange("p (c f) -> p c f", c=8)
            wus = rv[:, 2048:4096].rearrange("p (c f) -> p c f", c=8)
            wds = rv[:, 4096:6144].rearrange("p (j d) -> p j d", j=2)
            for t in range(NTB):
                ab = cnt["act"] % 2
                cnt["act"] += 1
                for jj in range(2):
                    pg, pu = jj, 2 + jj
                    for c in range(DC):
                        mm(ps[pg][:, :], wgs[:, c, jj * 128:(jj + 1) * 128], hT[:, c, tbs(t)], c == 0, c == DC - 1,
                           [r_ring[i], r_h[c][t]], [r_ps[pg]])
                    for c in range(DC):
                        mm(ps[pu][:, :], wus[:, c, jj * 128:(jj + 1) * 128], hT[:, c, tbs(t)], c == 0, c == DC - 1,
                           [r_ring[i], r_h[c][t]], [r_ps[pu]])
                    k = cnt["sg"] % 2
                    cnt["sg"] += 1
                    act(sg[k][:, :], ps[pg][:, :], AF.Silu, [r_ps[pg]], [r_sg[k]])
                    tt("dve", actb[ab][:, jj, :], sg[k][:, :], ps[pu][:, :], ALU.mult, [r_sg[k], r_ps[pu]], [r_act[ab][jj]])
                for c in range(DC):
                    pd = 4 + (pdi % 3)
                    pdi += 1
                    for jj in range(2):
                        mm(ps[pd][:, :], wds[:, jj, c * 128:(c + 1) * 128], actb[ab][:, jj, :], jj == 0, jj == 1,
                           [r_ring[i], r_act[ab][jj]], [r_ps[pd]])
                    stt("dve", x[:, c, tbs(t)], ps[pd][:, :], 0.5, x[:, c, tbs(t)], ALU.mult, ALU.add,
                        [r_ps[pd], r_x[c][t]], [r_x[c][t]])


    final_toks = []
    s_out = P.new_dma_sem("xout")

    def store_x():
        ov = xT_out.rearrange("(c p) n -> p c n", p=128)
        for c in range(DC):
            final_toks.append(P.dma("sp", s_out, lambda e, o=ov[:, c, :], i=x[:, c, :]: e.dma_start(out=o, in_=i),
                                    reads=[r_x[c][t] for t in range(NTB)], writes=[]))

    def finish():
        P.wait_all("sp", final_toks)
        P.wait_all("pool", final_toks)
        P.emit()
        st.close()
        return nc

    sqs = [sb(f"sqs{i}", [128, TB], BF16) for i in range(2)]
    r_sqs = [Res(), Res()]
    cnt["sqs"] = 0

    def qknorm(ps_ap, r_psrc, n, gcol, dst, r_dst, ssbank):
        k = cnt["sqs"] % 2
        cnt["sqs"] += 1
        act(sqs[k][:, 0:n], ps_ap, AF.Square, [r_psrc], [r_sqs[k]])
        mm(ps[ssbank][:, 0:n], bd_m, sqs[k][:, 0:n], True, True, [r_sqs[k], r_cmat], [r_ps[ssbank]])
        rs, r_rs = make_rstd(ps[ssbank][:, 0:n], r_ps[ssbank], n, 64.0)
        stt("dve", dst, ps_ap, gains[:, gcol:gcol + 1], rs[:, 0:n], ALU.mult, ALU.mult,
            [r_psrc, r_rs, r_gains], [r_dst])

    if phase == "A":
        ffn(wg_d, wu_d, wd_d, G_FFN1)
        store_x()
        for t in range(NTB):
            rmsnorm_tb(lambda c, t=t: x[:, c, tbs(t)], lambda c, t=t: r_x[c][t],
                       lambda c, t=t: hT[:, c, tbs(t)], lambda c, t=t: r_h[c][t], G_MIX, TB, 7)
        if mixer == "b":
            pos_i = sb("pos_i", [128, NT], I32)
            tf_ = sb("tf", [128, NT], F32)
            ti_ = pos_i
            cosT = sb("cosT", [128, NT], F32)
            tr_ = cosT
            sinT = sb("sinT", [128, NT], F32)
            invf = sb("invf_sb", [128, 1], F32)
            r_pos, r_tf, r_ti, r_tr, r_cos, r_sin, r_invf = [Res() for _ in range(7)]
            r_ti = r_pos
            r_tr = r_cos
            s_pos = P.new_dma_sem("pos")
            dma("sp", s_pos, pos_i[:], pos_d, [], [r_pos])
            dma("sp", s_pos, invf[:], invf_d, [], [r_invf])
            r_pos.w = r_invf.w = (s_pos, P.dma_val[s_pos])
            P.op("dve", lambda e: e.tensor_copy(out=tr_[:], in_=pos_i[:]), [r_pos], [r_tr])
            for which, dstT, r_dstT in ((0.0, sinT, r_sin), (0.25, cosT, r_cos)):
                P.op("dve", lambda e, w=which: e.tensor_scalar(out=tf_[:], in0=tr_[:], scalar1=invf[:, 0:1], scalar2=w, op0=ALU.mult, op1=ALU.add),
                     [r_tr, r_invf], [r_tf])
                P.op("dve", lambda e: e.tensor_copy(out=ti_[:], in_=tf_[:]), [r_tf], [r_ti])
                P.op("dve", lambda e, d=dstT: e.tensor_copy(out=d[:], in_=ti_[:]), [r_ti], [r_dstT])
                tt("dve", tf_[:], tf_[:], dstT[:], ALU.subtract, [r_tf, r_dstT], [r_tf])
                P.op("dve", lambda e, d=dstT: e.tensor_single_scalar(out=d[:], in_=tf_[:], scalar=0.5, op=ALU.is_gt), [r_tf], [r_dstT])
                tt("dve", tf_[:], tf_[:], dstT[:], ALU.subtract, [r_tf, r_dstT], [r_tf])
                P.op("dve", lambda e, d=dstT: e.tensor_single_scalar(out=d[:], in_=tf_[:], scalar=-0.5, op=ALU.is_lt), [r_tf], [r_dstT])
                tt("dve", tf_[:], tf_[:], dstT[:], ALU.add, [r_tf, r_dstT], [r_tf])
                act(dstT[:], tf_[:], AF.Sin, [r_tf], [r_dstT], scale=2.0 * np.pi * (1.0 - 2e-6))
            qn = [sb(f"qn{i}", [128, TB], BF16) for i in range(2)]
            t1_0 = sb("t1_0", [128, TB], F32); t2_0 = sb("t2_0", [128, TB], F32)
            t1 = [t1_0, t1_0]; t2 = [t2_0, t2_0]
            r_qn = [Res(), Res()]; r_t1_0 = Res(); r_t2_0 = Res(); r_t1 = [r_t1_0, r_t1_0]; r_t2 = [r_t2_0, r_t2_0]
        stage = [sb(f"stg{i}", [128, TB], BF16) for i in range(2)]
        r_stage = [Res(), Res()]
        s_stage = [P.new_dma_sem(f"stg{i}") for i in range(2)]
        vst = [sb(f"vst{i}", [128, 768], BF16) for i in range(2)]
        r_vst = [Res(), Res()]
        s_vst = [P.new_dma_sem(f"vst{i}") for i in range(2)]
        winv = win_d.rearrange("(c p) f -> p c f", p=128)
        pj = [0]

        def load_w(col0, ncol):
            i = ring_next()
            dma("pool", s_ring[i], ring[i][:, 0:8 * ncol].rearrange("p (c f) -> p c f", c=8), winv[:, :, col0:col0 + ncol], [], [r_ring[i]])
            return i, ring[i][:, 0:8 * ncol].rearrange("p (c f) -> p c f", c=8)

        def proj_fm(i, wv, col, gcol, rope, out_rows):
            for t in range(NTB):
                k = pj[0] % 2
                pj[0] += 1
                pb = k
                for c in range(DC):
                    mm(ps[pb][:, :], wv[:, c, col * 128:(col + 1) * 128], hT[:, c, tbs(t)], c == 0, c == DC - 1,
                       [r_ring[i], r_h[c][t]], [r_ps[pb]])
                if not rope:
                    qknorm(ps[pb][:, :], r_ps[pb], TB, gcol, stage[k][:, :], r_stage[k], 2 + k)
                else:
                    qknorm(ps[pb][:, :], r_ps[pb], TB, gcol, qn[k][:, :], r_qn[k], 2 + k)
                    mm(ps[4 + k][:, :], rot_m, qn[k][:, :], True, True, [r_qn[k], r_cmat], [r_ps[4 + k]])
                    tt("pool", t1[k][:, :], qn[k][:, :], cosT[:, tbs(t)], ALU.mult, [r_qn[k], r_cos], [r_t1[k]])
                    tt("dve", t2[k][:, :], ps[4 + k][:, :], sinT[:, tbs(t)], ALU.mult, [r_ps[4 + k], r_sin], [r_t2[k]])
                    tt("pool", stage[k][:, :], t1[k][:, :], t2[k][:, :], ALU.add, [r_t1[k], r_t2[k]], [r_stage[k]])
                final_toks.append(P.dma("sp", s_stage[k], lambda e, o=out_rows[:, tbs(t)], s_=stage[k]: e.dma_start(out=o, in_=s_[:, :]),
                                        reads=[r_stage[k]], writes=[]))

        rope = mixer == "b"
        i, wv = load_w(0, 768)
        i2, wv2 = load_w(768, 768)
        for col in range(6):
            proj_fm(i, wv, col, G_Q, rope, QT_out[col * 128:(col + 1) * 128, :])
        i3, wv3 = load_w(2304, 256)
        for col in range(6):
            proj_fm(i2, wv2, col, G_K, rope, KT_out[col * 128:(col + 1) * 128, :])
        i4, wv4 = load_w(1536, 768)
        for col in range(2):
            proj_fm(i3, wv3, col, G_MQ, False, QT_out[768 + col * 128:768 + (col + 1) * 128, :])
        for tl in range(16):
            k = tl % 2
            for half in range(2):
                pb = 6 + half
                for c in range(DC):
                    mm(ps[pb][:, 0:384], hT[:, c, tl * 128:(tl + 1) * 128], wv4[:, c, half * 384:(half + 1) * 384], c == 0, c == DC - 1,
                       [r_ring[i4], r_h[c][tl // 4]], [r_ps[pb]])
                act(vst[k][:, half * 384:(half + 1) * 384], ps[pb][:, 0:384], AF.Copy, [r_ps[pb]], [r_vst[k]])
            final_toks.append(P.dma("sp", s_vst[k], lambda e, o=V_out[tl * 128:(tl + 1) * 128, :], s_=vst[k]: e.dma_start(out=o, in_=s_[:, :]),
                                    reads=[r_vst[k]], writes=[]))
        return finish()


    qt = [sb(f"qt{i}", [128, NT], BF16) for i in range(2)]
    r_qt = [Res(), Res()]
    s_qt = [P.new_dma_sem(f"qt{i}") for i in range(2)]
    kt = sb("kt", [128, NKT * 128], BF16)
    r_kt = Res(); s_kt = P.new_dma_sem("kt")
    pT = [sb(f"pT{i}", [128, 512], BF16) for i in range(3)]
    r_pT = [Res(), Res(), Res()]
    orec = [sb(f"orec{i}", [128, 128], F32) for i in range(2)]
    r_orec = [Res(), Res()]
    cnt.update({"o": 0, "s": 0, "p": 0, "r": 0, "qt": 0})
    Vv = Vp_d.rearrange("(u p) f -> p u f", p=128)

    def load_qt(ch):
        k = cnt["qt"] % 2
        cnt["qt"] += 1
        dma("sp", s_qt[k], qt[k][:, :], QT_in[ch * 128:(ch + 1) * 128, :], [], [r_qt[k]])
        return qt[k], r_qt[k]

    def load_kt(ch):
        dma("sp", s_kt, kt[:, :], KTp_d[ch * 128:(ch + 1) * 128, :], [], [r_kt])

    memx = x[:, 0, :].rearrange("p (c n) -> p c n", c=DC)
    memh = sb("memh", [128, DC, 256], BF16)
    kmT = sb("kmT", [128, 2, 256], BF16)
    vmaug = sb("vmaug", [128, 2, 4, 128], BF16)
    r_memx, r_memh, r_kmT, r_vmaug = Res(), Res(), Res(), Res()
    s_mem = P.new_dma_sem("mem")
    dma("sp", s_mem, memx[:], memT_d.rearrange("(c p) n -> p c n", p=128), [], [r_memx])
    rmsnorm_tb(lambda c: memx[:, c, :], lambda c: r_memx, lambda c: memh[:, c, :], lambda c: r_memh, G_MEM, 256, 7)
    iw = ring_next()
    wkv = ring[iw][:, 0:4096].rearrange("p (c f) -> p c f", c=8)
    dma("pool", s_ring[iw], wkv, wkv_d.rearrange("(c p) f -> p c f", p=128), [], [r_ring[iw]])
    for ch in range(2):
        for c in range(DC):
            mm(ps[0][:, 0:256], wkv[:, c, ch * 128:(ch + 1) * 128], memh[:, c, :], c == 0, c == DC - 1, [r_ring[iw], r_memh], [r_ps[0]])
        qknorm(ps[0][:, 0:256], r_ps[0], 256, G_MK, kmT[:, ch, :], r_kmT, 2)
    P.op("pool", lambda e: e.memset(vmaug[:], 1.0), [], [r_vmaug])
    for nt_ in range(2):
        for c in range(DC):
            mm(ps[1][:, 0:256], memh[:, c, nt_ * 128:(nt_ + 1) * 128], wkv[:, c, 256:512], c == 0, c == DC - 1, [r_ring[iw], r_memh], [r_ps[1]])
        for hd in range(4):
            off = 0 if hd % 2 == 0 else 64
            act(vmaug[:, nt_, hd, off:off + 64], ps[1][:, hd * 64:(hd + 1) * 64], AF.Copy, [r_ps[1]], [r_vmaug])

    load_x([r_memx])

    class KTile:
        def __init__(self, k, v, bias=None, masks=(), reads=()):
            self.k, self.v, self.bias, self.masks, self.reads = k, v, bias, list(masks), list(reads)

    def attn_small(q_fn, r_q, keytiles, out_fn, r_out, r_bias=None):
        for e in (0, 1):
            ob = 2 + (cnt["o"] % 2)
            cnt["o"] += 1
            ng = (len(keytiles) + 3) // 4
            first = True
            for g in range(ng):
                grp = keytiles[g * 4:(g + 1) * 4]
                sbk = cnt["s"] % 2
                cnt["s"] += 1
                for t, kt_ in enumerate(grp):
                    cols = slice(t * 128, (t + 1) * 128)
                    extra = (1 if kt_.bias is not None else 0) + len(kt_.masks)
                    mm(ps[sbk][:, cols], kt_.k(e), q_fn(e), True, extra == 0, [r_q] + kt_.reads, [r_ps[sbk]])
                    n = 0
                    if kt_.bias is not None:
                        n += 1
                        mm(ps[sbk][:, cols], jrev_m, kt_.bias(e), False, n == extra, [r_cmat, r_bias], [r_ps[sbk]])
                    for (lrow, rrow) in kt_.masks:
                        n += 1
                        mm(ps[sbk][:, cols], lrow, rrow, False, n == extra, [r_rows], [r_ps[sbk]])
                w = len(grp) * 128
                pk = cnt["p"] % 3
                cnt["p"] += 1
                act(pT[pk][:, 0:w], ps[sbk][:, 0:w], AF.Exp, [r_ps[sbk]], [r_pT[pk]], scale=0.125)
                for t, kt_ in enumerate(grp):
                    last = (g == ng - 1 and t == len(grp) - 1)
                    mm(ps[ob][:, 0:128], kt_.v(e), pT[pk][:, t * 128:(t + 1) * 128], first, last, [r_pT[pk]] + kt_.reads, [r_ps[ob]])
                    first = False
            orows = slice(0, 64) if e == 0 else slice(64, 128)
            drows = slice(64, 128) if e == 0 else slice(0, 64)
            rk = cnt["r"] % 2
            cnt["r"] += 1
            recip(orec[rk][drows, :], ps[ob][drows, 0:128], [r_ps[ob]], [r_orec[rk]])
            tt("dve", out_fn(orows), ps[ob][orows, 0:128], orec[rk][drows, :], ALU.mult, [r_ps[ob], r_orec[rk]], [r_out])

    if mixer == "a":
        vaug = sb("vaug", [128, NKT, 2, 128], BF16)
        hb = sb("hb", [128, 12, 3, 128], BF16)
        e_sb = sb("e_sb", [12, 768], F32)
        r_vaug, r_hb, r_e, r_E = Res(), Res(), Res(), Res()
        s_v = P.new_dma_sem("vaug"); s_hb = P.new_dma_sem("hb"); s_e = P.new_dma_sem("esb"); s_E = P.new_dma_sem("Edram")
        P.op("pool", lambda e: e.memset(vaug[:], 1.0), [], [r_vaug])
        dma("sp", s_e, e_sb[:, 0:256], rb_d[:, 1:257], [], [r_e])
        P.op("dve", lambda e: e.memset(e_sb[:, 256:768], 0.0), [r_e], [r_e])
        P.op("dve", lambda e: e.tensor_scalar(out=e_sb[:, 256:768], in0=e_sb[:, 256:768], scalar1=e_sb[:, 255:256], scalar2=None, op0=ALU.add), [r_e], [r_e])
        P.op("dve", lambda e: e.tensor_scalar(out=e_sb[:, :], in0=e_sb[:, :], scalar1=8.0, scalar2=None, op0=ALU.mult), [r_e], [r_e])
        dma("sp", s_E, E_d, e_sb[:, :], [r_e], [r_E])
        for h_ in range(12):
            for jv, j in enumerate((2, 3, 4)):
                src = bass.AP(tensor=E_d.tensor, offset=h_ * 768 + (4 - j) * 128, ap=[[1, 128], [1, 128]])
                dma("pool", s_hb, hb[:, h_, jv, :], src, [r_E], [r_hb])
        for ch in range(6):
            qtb, r_q = load_qt(ch)
            load_kt(ch)
            for e in range(2):
                off = 0 if e == 0 else 64
                for u0 in range(0, NKT, 12):
                    dma("sp", s_v, vaug[:, u0:u0 + 12, e, off:off + 64], Vv[:, u0:u0 + 12, ch * 128 + e * 64:ch * 128 + e * 64 + 64], [], [r_vaug])
            for i in range(16):
                tiles = []
                for j in range(5):
                    u = 2 * i + j
                    masks = []
                    if j == 0:
                        masks.append((row(R_KLO), row(R_QHI)))
                    if j == 4:
                        masks.append((row(R_KHI), row(R_QLO)))
                    if i < 2 and j < 4:
                        masks.append((row(R_ONES), row(R_PAD0 + i * 4 + j)))
                    tiles.append(KTile(lambda e, u=u: kt[64 * e:64 * e + 64, u * 128:(u + 1) * 128],
                                       lambda e, u=u: vaug[:, u, e, :],
                                       lambda e, j=j, ch=ch: hb[:, 2 * ch + e, max(j - 2, 0), :], masks, [r_kt, r_vaug]))
                attn_small(lambda e, i=i, qtb=qtb: qtb[64 * e:64 * e + 64, i * 128:(i + 1) * 128], r_q, tiles,
                           lambda rws, i=i, ch=ch: hT[rws, ch, i * 128:(i + 1) * 128], r_h[ch][i // 4], r_hb)
    else:
        vb = sb("vb", [128, NKT, 128], BF16)
        om = [sb(f"om{i}", [128, 128], F32) for i in range(2)]
        od = sb("od", [128, 128], F32)
        lamv = sb("lamv_sb", [1, 256], F32)
        lconst = sb("lconst_sb", [128, 2], F32)
        lam1 = sb("lam1", [1, 4], F32)
        onesf = sb("onesf", [1, 128], F32)
        neglam = sb("neglam", [128, 1], F32)
        gsub2 = sb("gsub2", [128, 1], F32)
        r_vb, r_om0, r_om1, r_od, r_lam, r_lc, r_neg, r_gs2 = [Res() for _ in range(8)]
        r_om = [r_om0, r_om1]
        s_v = P.new_dma_sem("vb"); s_l = P.new_dma_sem("lam")
        dma("sp", s_l, lamv[:], lamv_d, [], [r_lam])
        dma("sp", s_l, lconst[:], lconst_d, [], [r_lc])
        r_lam.w = r_lc.w = (s_l, P.dma_val[s_l])
        P.op("dve", lambda e: e.memset(onesf[:], 1.0), [], [r_lam])
        tt("dve", lamv[0:1, 0:64], lamv[0:1, 0:64], lamv[0:1, 64:128], ALU.mult, [r_lam], [r_lam])
        tt("dve", lamv[0:1, 128:192], lamv[0:1, 128:192], lamv[0:1, 192:256], ALU.mult, [r_lam], [r_lam])
        P.op("dve", lambda e: e.reduce_sum(out=lam1[0:1, 0:1], in_=lamv[0:1, 0:64], axis=mybir.AxisListType.X), [r_lam], [r_lam])
        P.op("dve", lambda e: e.reduce_sum(out=lam1[0:1, 1:2], in_=lamv[0:1, 128:192], axis=mybir.AxisListType.X), [r_lam], [r_lam])
        act(lam1[0:1, 0:2], lam1[0:1, 0:2], AF.Exp, [r_lam], [r_lam])
        tt("dve", lam1[0:1, 2:3], lam1[0:1, 1:2], lam1[0:1, 0:1], ALU.subtract, [r_lam], [r_lam])
        tt("dve", lam1[0:1, 3:4], lam1[0:1, 2:3], lconst[0:1, 0:1], ALU.subtract, [r_lam, r_lc], [r_lam])
        mm(ps[6][:, 0:1], onesf[0:1, :], lam1[0:1, 3:4], True, True, [r_lam], [r_ps[6]])
        P.op("dve", lambda e: e.tensor_copy(out=neglam[:], in_=ps[6][:, 0:1]), [r_ps[6]], [r_neg])
        tt("dve", gsub2[:], gains[:, G_SUB:G_SUB + 1], lconst[:, 1:2], ALU.mult, [r_gains, r_lc], [r_gs2])
        for hd in range(6):
            qtb, r_q = load_qt(hd)
            load_kt(hd)
            for u0 in range(0, NKT, 12):
                dma("sp", s_v, vb[:, u0:u0 + 12, :], Vv[:, u0:u0 + 12, hd * 128:(hd + 1) * 128], [], [r_vb])
            for i in range(16):
                us = list(range(3, 2 * i + 5))
                for m in (0, 1):
                    ob = 2 + (cnt["o"] % 2)
                    dbk = 4 + (cnt["o"] % 2)
                    cnt["o"] += 1
                    ng = (len(us) + 3) // 4
                    first = True
                    for g in range(ng):
                        grp = us[g * 4:(g + 1) * 4]
                        sbk = cnt["s"] % 2
                        cnt["s"] += 1
                        for t, u in enumerate(grp):
                            cols = slice(t * 128, (t + 1) * 128)
                            masks = []
                            if u == 3:
                                masks.append((row(R_ONES), row(R_B3)))
                            if u == 2 * i + 4:
                                masks.append((row(R_KHI), row(R_QLO)))
                            mm(ps[sbk][:, cols], kt[64 * m:64 * m + 64, u * 128:(u + 1) * 128], qtb[64 * m:64 * m + 64, i * 128:(i + 1) * 128],
                               True, len(masks) == 0, [r_q, r_kt], [r_ps[sbk]])
                            for n, (lrow, rrow) in enumerate(masks):
                                mm(ps[sbk][:, cols], lrow, rrow, False, n == len(masks) - 1, [r_rows], [r_ps[sbk]])
                        w = len(grp) * 128
                        pk = cnt["p"] % 3
                        cnt["p"] += 1
                        act(pT[pk][:, 0:w], ps[sbk][:, 0:w], AF.Exp, [r_ps[sbk]], [r_pT[pk]], scale=0.125)
                        for t, u in enumerate(grp):
                            last = (g == ng - 1 and t == len(grp) - 1)
                            mm(ps[ob][:, 0:128], vb[:, u, :], pT[pk][:, t * 128:(t + 1) * 128], first, last, [r_pT[pk], r_vb], [r_ps[ob]])
                            mm(ps[dbk][:, 0:128], ones_m, pT[pk][:, t * 128:(t + 1) * 128], first, last, [r_pT[pk], r_cmat], [r_ps[dbk]])
                            first = False
                    rk = cnt["r"] % 2
                    cnt["r"] += 1
                    recip(orec[rk][:, :], ps[dbk][:, 0:128], [r_ps[dbk]], [r_orec[rk]])
                    tt("dve", om[m][:, :], ps[ob][:, 0:128], orec[rk][:, :], ALU.mult, [r_ps[ob], r_orec[rk]], [r_om[m]])
                stt("dve", od[:, :], om[1][:, :], neglam[:, 0:1], om[0][:, :], ALU.mult, ALU.add, [r_om[0], r_om[1], r_neg], [r_od])
                k = cnt["sqs"] % 2
                cnt["sqs"] += 1
                act(sqs[k][:, 0:128], od[:, :], AF.Square, [r_od], [r_sqs[k]])
                mm(ps[6][:, 0:128], ones_m, sqs[k][:, 0:128], True, True, [r_sqs[k], r_cmat], [r_ps[6]])
                rs, r_rs = make_rstd(ps[6][:, 0:128], r_ps[6], 128, 128.0)
                stt("dve", hT[:, hd, i * 128:(i + 1) * 128], od[:, :], gsub2[:, 0:1], rs[:, 0:128], ALU.mult, ALU.mult,
                    [r_od, r_rs, r_gs2], [r_h[hd][i // 4]])

    for mc in range(2):
        qtb, r_q = load_qt(6 + mc)
        for i in range(16):
            tiles = [KTile(lambda e, nt_=nt_, mc=mc: kmT[64 * e:64 * e + 64, mc, nt_ * 128:(nt_ + 1) * 128],
                           lambda e, nt_=nt_, mc=mc: vmaug[:, nt_, 2 * mc + e, :], None, [], [r_kmT, r_vmaug]) for nt_ in range(2)]
            attn_small(lambda e, i=i, qtb=qtb: qtb[64 * e:64 * e + 64, i * 128:(i + 1) * 128], r_q, tiles,
                       lambda rws, i=i, mc=mc: hT[rws, 6 + mc, i * 128:(i + 1) * 128], r_h[6 + mc][i // 4])

    woutv = wout_d.rearrange("(j p) d -> p j d", p=128)
    wi = 0
    for half in range(2):
        iw = ring_next()
        wo = ring[iw][:, 0:4096].rearrange("p (j d) -> p j d", j=8)
        dma("pool", s_ring[iw], wo, woutv[:, :, half * 512:(half + 1) * 512], [], [r_ring[iw]])
        for cc in range(4):
            c = half * 4 + cc
            for t in range(NTB):
                bk = 6 + (wi % 2)
                wi += 1
                for j in range(DC):
                    mm(ps[bk][:, :], wo[:, j, cc * 128:(cc + 1) * 128], hT[:, j, tbs(t)], j == 0, j == DC - 1, [r_ring[iw], r_h[j][t]], [r_ps[bk]])
                tt("dve", x[:, c, tbs(t)], ps[bk][:, :], x[:, c, tbs(t)], ALU.add, [r_ps[bk], r_x[c][t]], [r_x[c][t]])

    ffn(wg_d, wu_d, wd_d, G_FFN2)
    store_x()
    return finish()


import ml_dtypes
_BF = ml_dtypes.bfloat16
N_CORES = 8
_PROGS = {}


def _prog(phase, mixer):
    k = (phase, mixer)
    if k not in _PROGS:
        _PROGS[k] = build_program(phase, mixer)
    return _PROGS[k]


def _tok_index(h):
    i = np.arange(16)[:, None]
    j = np.arange(128)[None, :]
    return ((2 * i + h) * 128 + j).reshape(-1)


def _const_mats():
    ones = np.ones((128, 128), np.float32)
    bd = np.zeros((128, 128), np.float32)
    bd[:64, :64] = 1.0
    bd[64:, 64:] = 1.0
    rot = np.zeros((128, 128), np.float32)
    for p in range(128):
        if p % 64 < 32:
            rot[p + 32, p] = -1.0
        else:
            rot[p - 32, p] = 1.0
    jrev = np.zeros((128, 128), np.float32)
    for p in range(128):
        jrev[p, 127 - p] = 1.0
    return np.concatenate([ones, bd, rot, jrev], axis=1).astype(_BF)


def _rows(h):
    r = np.zeros((NROW, 128), np.float32)
    k = np.arange(128)
    r[R_ONES] = 1.0
    r[R_KLO] = (k < 64)
    r[R_KHI] = (k >= 64)
    r[R_QLO] = MASKV * (k < 64)
    r[R_QHI] = MASKV * (k >= 64)
    r[R_QALL] = MASKV
    for i in range(2):
        for j in range(4):
            if 2 * i + h - 4 + j < 0:
                r[R_PAD0 + i * 4 + j] = MASKV
    if h == 0:
        r[R_B3] = MASKV
    return r.reshape(1, -1).astype(_BF)


def _col8(v):
    return np.ascontiguousarray(np.asarray(v, np.float32).reshape(8, 128).T)


def _dup(v):
    v = np.asarray(v, np.float32)
    return np.concatenate([v, v]).reshape(128, 1)


def _gains(inp, l):
    j = l // 2
    mixer = "a" if l % 2 == 0 else "b"
    g = np.zeros((128, NGC), np.float32)
    g[:, G_FFN1:G_FFN1 + 8] = _col8(inp["ffn1_norm"][l])
    g[:, G_MIX:G_MIX + 8] = _col8(inp["mix_norm"][l])
    g[:, G_FFN2:G_FFN2 + 8] = _col8(inp["ffn2_norm"][l])
    g[:, G_MEM:G_MEM + 8] = _col8(inp["mem_norm"][l])
    g[:, G_Q:G_Q + 1] = _dup(inp[mixer + "_q_norm"][j])
    g[:, G_K:G_K + 1] = _dup(inp[mixer + "_k_norm"][j])
    g[:, G_MQ:G_MQ + 1] = _dup(inp["mem_q_norm"][l])
    g[:, G_MK:G_MK + 1] = _dup(inp["mem_k_norm"][l])
    if mixer == "b":
        g[:, G_SUB] = np.asarray(inp["b_subln"][j], np.float32)
    return g


def _invf():
    inv = (1.0 / (10000.0 ** (np.arange(0, 64, 2, dtype=np.float32) / 64.0))).astype(np.float32)
    v = (inv.astype(np.float64) / (2.0 * np.pi)).astype(np.float32)
    return np.tile(v, 4).reshape(128, 1)


def run_phase_A(inp, l, xT_cores):
    mixer = "a" if l % 2 == 0 else "b"
    nc = _prog("A", mixer)
    cm = _const_mats()
    g = _gains(inp, l)
    wg = np.ascontiguousarray(inp["ffn1_w_gate"][l]); wu = np.ascontiguousarray(inp["ffn1_w_up"][l])
    wd = np.ascontiguousarray(inp["ffn1_w_down"][l]); win = np.ascontiguousarray(inp["w_in"][l])
    maps = []
    for c in range(N_CORES):
        b, h = c // 2, c % 2
        m = {"xT": xT_cores[c], "gains": g, "cmat": cm, "rows": _rows(h), "wg": wg, "wu": wu, "wd": wd, "w_in": win}
        if mixer == "b":
            pos = np.asarray(inp["positions"])[b, _tok_index(h)].astype(np.int32)
            m["pos"] = np.ascontiguousarray(np.broadcast_to(pos[None, :], (128, NT)))
            m["invf"] = _invf()
        maps.append(m)
    res = run_bass_kernel_spmd(nc, maps, core_ids=list(range(N_CORES)))
    return res.results


def run_phase_B(inp, l, resA):
    import math
    mixer = "a" if l % 2 == 0 else "b"
    j = l // 2
    nc = _prog("B", mixer)
    cm = _const_mats()
    g = _gains(inp, l)
    wg = np.ascontiguousarray(inp["ffn2_w_gate"][l]); wu = np.ascontiguousarray(inp["ffn2_w_up"][l])
    wd = np.ascontiguousarray(inp["ffn2_w_down"][l])
    wout = np.ascontiguousarray(inp["w_out"][l]); wkv = np.ascontiguousarray(inp["w_mem_kv"][l])
    maps = []
    for b in range(4):
        KTfull = np.zeros((768, 32 * 128), _BF)
        Vfull = np.zeros((32 * 128, 768), _BF)
        for T in range(32):
            src, loc = 2 * b + T % 2, T // 2
            KTfull[:, T * 128:(T + 1) * 128] = resA[src]["KT"][:, loc * 128:(loc + 1) * 128]
            Vfull[T * 128:(T + 1) * 128, :] = resA[src]["V"][loc * 128:(loc + 1) * 128, :]
        for h in range(2):
            c = 2 * b + h
            KTp = np.zeros((768, NKT * 128), _BF)
            Vp = np.zeros((NKT * 128, 768), _BF)
            lo = max(0, 4 - h)
            hi = min(NKT, 32 + 4 - h)
            KTp[:, lo * 128:hi * 128] = KTfull[:, (lo - 4 + h) * 128:(hi - 4 + h) * 128]
            Vp[lo * 128:hi * 128, :] = Vfull[(lo - 4 + h) * 128:(hi - 4 + h) * 128, :]
            m = {"xT": resA[c]["xT_out"], "QT": resA[c]["QT"], "KTp": KTp, "Vp": Vp,
                 "memT": np.ascontiguousarray(np.asarray(inp["mem"])[b].T),
                 "gains": g, "cmat": cm, "rows": _rows(h), "wg": wg, "wu": wu, "wd": wd, "w_out": wout, "w_kv": wkv}
            if mixer == "a":
                m["relb"] = np.ascontiguousarray(inp["a_rel_bias"][j])
            else:
                m["lamv"] = np.concatenate([inp["b_lambda_q1"][j], inp["b_lambda_k1"][j],
                                            inp["b_lambda_q2"][j], inp["b_lambda_k2"][j]]).astype(np.float32).reshape(1, 256)
                li = 0.8 - 0.6 * math.exp(-0.3 * l)
                m["lconst"] = np.tile(np.array([[li, 1.0 - li]], np.float32), (128, 1))
            maps.append(m)
    res = run_bass_kernel_spmd(nc, maps, core_ids=list(range(N_CORES)))
    return res.results


def kernel(**inputs):
    inp = {k: np.asarray(v) for k, v in inputs.items()}
    x = inp["x"]
    xT_cores = []
    for c in range(N_CORES):
        b, h = c // 2, c % 2
        xT_cores.append(np.ascontiguousarray(x[b, _tok_index(h), :].T))
    for l in range(4):
        resA = run_phase_A(inp, l, xT_cores)
        resB = run_phase_B(inp, l, resA)
        xT_cores = [resB[c]["xT_out"] for c in range(N_CORES)]
    out = np.zeros(x.shape, np.float32)
    for c in range(N_CORES):
        b, h = c // 2, c % 2
        out[b, _tok_index(h), :] = xT_cores[c].T
    return out
```
